# Optimizing a Trainium2 kernel written in Bass

```python
import math
import jax, jax.numpy as jnp
from jax import lax
import numpy as np

D_MODEL = 1024
BATCH = 1
SEQ = 16384
DEPTH = 1
DEC_BATCH = 32
DEC_SEQ = 64
PAST_LEN = 4096

CHUNK = 64
RET_HEADS = 8
RET_DK = 128
RET_DV = 128
RET_WIDTH = RET_HEADS * RET_DV
D_CONV = 1024
CONV_W = 3
D_FF = 4 * D_MODEL
ROPE_BASE = 10000.0
EPS = 1e-6
IN_SIZES = (RET_HEADS * RET_DK, RET_HEADS * RET_DK, RET_WIDTH, RET_WIDTH, D_CONV, D_CONV, D_CONV, D_MODEL, D_MODEL)
D_IN = sum(IN_SIZES)

kernel_name = "hybrid_retention_shortconv_stream_step"


def rmsnorm(x, w):
    xf = x.astype(jnp.float32)
    y = xf * lax.rsqrt(jnp.mean(xf * xf, axis=-1, keepdims=True) + EPS)
    return (y * w.astype(jnp.float32)).astype(x.dtype)


def rope(x, pos):
    half = x.shape[-1] // 2
    inv = jnp.exp(-math.log(ROPE_BASE) * jnp.arange(half, dtype=jnp.float32) / half)
    ang = pos[:, None] * inv[None, :]
    cos = jnp.cos(ang)[None, :, None, :]
    sin = jnp.sin(ang)[None, :, None, :]
    xf = x.astype(jnp.float32)
    x1, x2 = xf[..., :half], xf[..., half:]
    return jnp.concatenate([x1 * cos - x2 * sin, x2 * cos + x1 * sin], axis=-1)


def log_gammas():
    h = jnp.arange(RET_HEADS, dtype=jnp.float32)
    return jnp.log(1.0 - jnp.exp2(-5.0 - h))


def retention_chunk(S, q, k, v):
    L = q.shape[1]
    lg = log_gammas()
    i = jnp.arange(L, dtype=jnp.float32)
    intra = jnp.exp(lg[:, None, None] * jnp.abs(i[:, None] - i[None, :]))
    scores = jnp.einsum('blhd,bmhd->bhlm', q, k) * intra[None]
    o_intra = jnp.einsum('bhlm,bmhe->blhe', scores, v)
    inter_decay = jnp.exp(lg[None, :] * (i[:, None] + 1.0))
    o_inter = jnp.einsum('blhd,bhde->blhe', q, S) * inter_decay[None, :, :, None]
    kv_decay = jnp.exp(lg[None, :] * (L - 1.0 - i[:, None]))
    S_new = S * jnp.exp(lg * L)[None, :, None, None] + jnp.einsum('blhd,blhe->bhde', k * kv_decay[None, :, :, None], v)
    return S_new, o_intra + o_inter


def retention(S, q, k, v):
    b, L = q.shape[0], q.shape[1]
    if L <= CHUNK:
        return retention_chunk(S, q, k, v)
    nc = L // CHUNK
    def to_chunks(t):
        return jnp.swapaxes(t.reshape(b, nc, CHUNK, t.shape[2], t.shape[3]), 0, 1)
    S_fin, o = lax.scan(lambda s, xs: retention_chunk(s, *xs), S, (to_chunks(q), to_chunks(k), to_chunks(v)))
    o = jnp.swapaxes(o, 0, 1).reshape(b, L, RET_HEADS, RET_DV)
    return S_fin, o


def layer(x, pos, s_ret, conv_buf, norm1, w_in, ret_gn_w, conv_w, w_ret_out, w_conv_out, w_o, norm2, w_ff1, w_ff2):
    b, l, _ = x.shape
    xn = rmsnorm(x, norm1)
    proj = xn @ w_in
    splits = np.cumsum(IN_SIZES)[:-1].tolist()
    q, k, v, g, cb, cc, cx, ga, gb = jnp.split(proj, splits, axis=-1)
    qh = rope(q.reshape(b, l, RET_HEADS, RET_DK), pos)
    kh = rope(k.reshape(b, l, RET_HEADS, RET_DK), pos) * (RET_DK ** -0.5)
    vh = v.reshape(b, l, RET_HEADS, RET_DV).astype(jnp.float32)
    s_new, o = retention(s_ret.astype(jnp.float32), qh, kh, vh)
    mu = jnp.mean(o, axis=-1, keepdims=True)
    var = jnp.mean(jnp.square(o - mu), axis=-1, keepdims=True)
    o = ((o - mu) * lax.rsqrt(var + EPS)).reshape(b, l, RET_WIDTH) * ret_gn_w.astype(jnp.float32)
    ret_y = (o * jax.nn.silu(g.astype(jnp.float32))).astype(x.dtype) @ w_ret_out
    u = cc * cx
    full = jnp.concatenate([conv_buf.astype(u.dtype), u], axis=1)
    y = sum(conv_w[j] * full[:, j:j + l] for j in range(CONV_W))
    new_buf = full[:, l:]
    conv_y = (cb * y) @ w_conv_out
    mix = jax.nn.sigmoid(ga) * ret_y + jax.nn.sigmoid(gb) * conv_y
    h = x + mix @ w_o
    hn = rmsnorm(h, norm2)
    h = h + jnp.square(jax.nn.relu(hn @ w_ff1)) @ w_ff2
    return h, s_new, new_buf


def setup_inputs(seed: int = 0) -> dict:
    key = jax.random.key(seed)
    ks = jax.random.split(key, 16)
    f32 = jnp.float32
    def nrm(k, shape, scale):
        return jax.random.normal(k, shape, f32) * scale
    return {
        "x_prompt": nrm(ks[0], (BATCH, SEQ, D_MODEL), 1.0),
        "x_sample": nrm(ks[1], (DEC_BATCH, DEC_SEQ, D_MODEL), 1.0),
        "state_ret": nrm(ks[2], (DEPTH, DEC_BATCH, RET_HEADS, RET_DK, RET_DV), 1.0),
        "state_conv": nrm(ks[3], (DEPTH, DEC_BATCH, CONV_W - 1, D_CONV), 1.0),
        "norm1": 1.0 + nrm(ks[4], (DEPTH, D_MODEL), 0.02),
        "w_in": nrm(ks[5], (DEPTH, D_MODEL, D_IN), D_MODEL ** -0.5),
        "ret_gn_w": 1.0 + nrm(ks[6], (DEPTH, RET_WIDTH), 0.02),
        "conv_w": nrm(ks[7], (DEPTH, CONV_W, D_CONV), CONV_W ** -0.5),
        "w_ret_out": nrm(ks[8], (DEPTH, RET_WIDTH, D_MODEL), RET_WIDTH ** -0.5),
        "w_conv_out": nrm(ks[9], (DEPTH, D_CONV, D_MODEL), D_CONV ** -0.5),
        "w_o": nrm(ks[10], (DEPTH, D_MODEL, D_MODEL), D_MODEL ** -0.5),
        "norm2": 1.0 + nrm(ks[11], (DEPTH, D_MODEL), 0.02),
        "w_ff1": nrm(ks[12], (DEPTH, D_MODEL, D_FF), D_MODEL ** -0.5),
        "w_ff2": nrm(ks[13], (DEPTH, D_FF, D_MODEL), D_FF ** -0.5),
        "norm_f": 1.0 + nrm(ks[14], (D_MODEL,), 0.02),
    }


def reference(x_prompt, x_sample, state_ret, state_conv, norm1, w_in, ret_gn_w, conv_w, w_ret_out, w_conv_out, w_o, norm2, w_ff1, w_ff2, norm_f):
    bp, lp = x_prompt.shape[0], x_prompt.shape[1]
    ls = x_sample.shape[1]
    pos_p = jnp.arange(lp, dtype=jnp.float32)
    pos_s = PAST_LEN + jnp.arange(ls, dtype=jnp.float32)
    yp, ys = x_prompt, x_sample
    ret_p, conv_p, ret_s, conv_s = [], [], [], []
    for d in range(DEPTH):
        w = (norm1[d], w_in[d], ret_gn_w[d], conv_w[d], w_ret_out[d], w_conv_out[d], w_o[d], norm2[d], w_ff1[d], w_ff2[d])
        s0 = jnp.zeros((bp, RET_HEADS, RET_DK, RET_DV), jnp.float32)
        b0 = jnp.zeros((bp, CONV_W - 1, D_CONV), x_prompt.dtype)
        yp, sp, cp = layer(yp, pos_p, s0, b0, *w)
        ys, ss, cs = layer(ys, pos_s, state_ret[d], state_conv[d], *w)
        ret_p.append(sp.astype(state_ret.dtype))
        conv_p.append(cp.astype(state_conv.dtype))
        ret_s.append(ss.astype(state_ret.dtype))
        conv_s.append(cs.astype(state_conv.dtype))
    yp = rmsnorm(yp, norm_f)
    ys = rmsnorm(ys, norm_f)
    return (yp, ys, jnp.stack(ret_p), jnp.stack(conv_p), jnp.stack(ret_s), jnp.stack(conv_s))
```

```python
from contextlib import ExitStack

import numpy as np
import concourse.bass as bass
import concourse.mybir as mybir
from concourse.bass_utils import run_bass_kernel_spmd

F32 = mybir.dt.float32
BF16 = mybir.dt.bfloat16
AF = mybir.ActivationFunctionType
ALU = mybir.AluOpType
AX = mybir.AxisListType

NCORES = 8
D = 1024
H = 8
DIN = 9216
DFF = 4096
SEQ = 16384
DEC_B = 32
DEC_S = 64
PAST = 4096
EPS = 1e-6
ROPE_BASE = 10000.0
TPC = 2304
NT = 18
NPT = 16
G = 6
NG = 3
UC = 778

T_N1, T_N2, T_GNW, T_CW, T_VP, T_VS, T_VA, T_VB, T_EP, T_ES, T_COEF = 0, 8, 16, 24, 48, 56, 64, 72, 80, 88, 96
NTAB = 160

W_IN_ORDER = ("q", "k", "v", "g", "cb", "cc", "cx", "ga", "gb")
STAGES = ["cx", "cc", "cb", "q", "k", "v", "g", "ro", "ga", "gb", "co", "o",
          "f10", "f20", "f11", "f21", "f12", "f22", "f13", "f23"]


class _Op:
    __slots__ = ("fn", "deps", "needs_inc", "count", "dma", "inc")

    def __init__(self, fn, deps, dma, inc):
        self.fn = fn
        self.deps = deps
        self.needs_inc = False
        self.count = 0
        self.dma = dma
        self.inc = inc


SEM_CAP = 960


class Sched:
    ENGS = ("pe", "act", "dve", "pool", "sp")

    def __init__(self):
        self.ops = {e: [] for e in self.ENGS}
        self.lastw = {}
        self.readers = {}
        self.waited = {e: {} for e in self.ENGS}
        self.dma_cnt = {}
        self.alias = {}
        self.region = {}
        self.final_dma = {}
        self.dma_epoch = {}

    def set_alias(self, name, region, owner):
        self.alias[name] = (region, owner)

    def add(self, eng, fn, reads=(), writes=(), dma=None, inc=16, final=False, record=True):
        idx = len(self.ops[eng])
        deps = {}

        def need(ev, allow_self):
            if ev is None:
                return
            src, val = ev
            if src == eng and eng in ("pe", "sp"):
                return
            if deps.get(src, -1) < val:
                deps[src] = val

        for k in reads:
            need(self.lastw.get(k), True)
        for k in writes:
            need(self.lastw.get(k), False)
            for ev in self.readers.get(k, ()):
                need(ev, False)
        touched = []
        for k in tuple(reads) + tuple(writes):
            name = k[0]
            if name in self.alias:
                reg, owner = self.alias[name]
                R = self.region.setdefault(reg, {"owner": owner, "users": {}, "fence": {}})
                if R["owner"] != owner:
                    f = dict(R["fence"])
                    for s, v in R["users"].items():
                        if f.get(s, -1) < v:
                            f[s] = v
                    R["fence"] = f
                    R["users"] = {}
                    R["owner"] = owner
                for s, v in R["fence"].items():
                    need((s, v), False)
                touched.append(R)
        w = self.waited[eng]
        fdeps = []
        for src, val in deps.items():
            if w.get(src, -1) >= val:
                continue
            w[src] = val
            fdeps.append((src, val))
            if src in self.ops:
                self.ops[src][val].needs_inc = True
        op = _Op(fn, fdeps, dma, inc)
        self.ops[eng].append(op)
        if dma is not None:
            ep = self.dma_epoch.get(dma, 0)
            if self.dma_cnt.get("%s#%d" % (dma, ep), 0) + inc > SEM_CAP:
                ep += 1
                self.dma_epoch[dma] = ep
            dma = "%s#%d" % (dma, ep)
            c = self.dma_cnt.get(dma, 0) + inc
            self.dma_cnt[dma] = c
            op.dma = dma
            ev = ("D:" + dma, c)
            if final:
                self.final_dma[dma] = c
        else:
            ev = (eng, idx)
        for k in (writes if record else ()):
            self.lastw[k] = ev
            self.readers[k] = []
        for k in reads:
            lst = self.readers.setdefault(k, [])
            lst[:] = [e for e in lst if e[0] != ev[0]]
            lst.append(ev)
        for R in touched:
            if R["users"].get(ev[0], -1) < ev[1]:
                R["users"][ev[0]] = ev[1]
        return ev

    def emit(self, nc, es):
        dsems = {k: es.enter_context(nc.semaphore("d_" + k.replace("#", "_"))) for k in self.dma_cnt}
        sems = {}
        for e in self.ENGS:
            c = 0
            ep = 0
            for op in self.ops[e]:
                if op.needs_inc:
                    if c >= SEM_CAP:
                        ep += 1
                        c = 0
                    c += 1
                op.count = (ep, c)
            sems[e] = [es.enter_context(nc.semaphore("se_%s%d" % (e, i))) for i in range(ep + 1)]
        block = es.enter_context(nc.Block())
        ops = self.ops
        final = self.final_dma

        def run(eng_name, h):
            for op in ops[eng_name]:
                for src, val in op.deps:
                    if src in ops:
                        ep_, c_ = ops[src][val].count
                        h.wait_ge(sems[src][ep_], c_)
                    else:
                        h.wait_ge(dsems[src[2:]], val)
                ins = op.fn(h)
                if op.dma is not None:
                    if op.inc == 16:
                        ins.then_inc(dsems[op.dma], 16)
                    else:
                        ins.then_inc(dsems[op.dma])
                elif op.needs_inc:
                    ins.then_inc(sems[eng_name][op.count[0]], 1)
            if eng_name == "sp":
                for k, c in final.items():
                    h.wait_ge(dsems[k], c)

        @block.tensor
        def _(h):
            run("pe", h)

        @block.scalar
        def _(h):
            run("act", h)

        @block.vector
        def _(h):
            run("dve", h)

        @block.gpsimd
        def _(h):
            run("pool", h)

        @block.sync
        def _(h):
            run("sp", h)


def _gammas():
    return [1.0 - 2.0 ** (-5.0 - h) for h in range(H)]


class _Stop(Exception):
    pass


def build(dbg=(), stop=None):
    dbg = set(dbg)
    nc = bass.Bass("TRN2", target_bir_lowering=False)
    S = Sched()
    es = ExitStack()

    def din(name, shape, dt=F32):
        return nc.dram_tensor(name, shape, dt, kind="ExternalInput").ap()

    def dout(name, shape, dt=F32):
        return nc.dram_tensor(name, shape, dt, kind="ExternalOutput").ap()

    x_d = din("x", [TPC, D])
    xh_d = din("xh", [2, D])
    sret_d = din("sret", [4, H, 128, 128])
    sconv_d = din("sconv", [4, 2, D])
    win_d = din("w_in", [D, DIN])
    wro_d = din("w_ro", [D, D])
    wco_d = din("w_co", [D, D])
    wo_d = din("w_o", [D, D])
    wf1_d = din("w_ff1", [D, DFF])
    wf2_d = din("w_ff2", [DFF, D])
    tabs_d = din("tabs", [128, NTAB])
    rope_d = din("rope", [128, 4, NT, 64])
    mask_d = din("masks", [128, 2, H, 128])
    nf_d = din("nf", [128, D])
    id_d = din("identf", [128, 128])
    NPREV = 7 * NPT
    NXP = 4
    xprev_d = [din("xprev%d" % i, [(NPREV // NXP) * 128, D]) for i in range(NXP)]
    ropeP_d = din("ropeP", [128, 2, NPREV, 64])

    y_d = dout("y", [TPC, D])
    nrp_d = dout("nrp", [H, 128, 128])
    ncp_d = dout("ncp", [2, D])
    nrs_d = dout("nrs", [4, H, 128, 128])
    ncs_d = dout("ncs", [4, 2, D])

    def sb(name, shape, dt):
        return es.enter_context(nc.sbuf_tensor("sb_" + name, shape, dt))

    ring = [sb("wr%d" % i, [128, 8, 1024], BF16) for i in range(3)]
    R1 = sb("R1", [128, 8 * UC], F32)
    R2 = sb("R2", [128, 8 * UC], F32)
    R3 = sb("R3", [128, 3072], F32)
    R4 = sb("R4", [128, 6144], BF16)
    SCR = sb("SCR", [128, 3072], F32)
    tabs = sb("tabs", [128, NTAB], F32)
    ropeb = sb("ropeb", [128, 4, G, 64], F32)
    maskb = sb("maskb", [128, 2, H, 128], F32)
    nfb = sb("nfb", [128, D], F32)
    identf = sb("identf", [128, 128], F32)
    ident = sb("ident", [128, 128], BF16)
    Sst = sb("Sst", [128, H, 128], F32)
    Sbf = sb("Sbf", [128, H, 128], BF16)
    sst = [sb("sst%d" % i, [128, H, 128], F32) for i in range(2)]
    sstb = [sb("sstb%d" % i, [128, H, 128], BF16) for i in range(2)]
    qTm = [sb("qTm%d" % i, [128, H, 128], BF16) for i in range(2)]
    xs = [sb("xs%d" % i, [128, D], F32) for i in range(2)]
    xnb = sb("xnb", [128, D], BF16)
    vdb = [sb("vd%d" % i, [128, H, 128], BF16) for i in range(3)]
    Pm = sb("Pm", [128, H, 128], BF16)
    tmp = [sb("tmp%d" % i, [128, 384], F32) for i in range(2)]
    st = sb("st", [128, 128], F32)
    xnTh = sb("xnTh", [128, 8, 2], BF16)
    hx = sb("hx", [128, 8, 2], F32)
    uh = sb("uh", [128, 8, 2], F32)

    ps = [es.enter_context(nc.psum_tensor("psum%d" % i, [128, 1024], F32)) for i in range(4)]

    r1, r2, r3, r4, scr = R1[:], R2[:], R3[:], R4[:], SCR[:]
    U = r1.rearrange("p (j c) -> p j c", j=8)
    qT = r1[:, 0:3072].bitcast(BF16).rearrange("p (h t) -> p h t", h=8)
    kr = r1[:, 3072:6144].bitcast(BF16).rearrange("p (t c) -> p t c", t=G)
    ry = r1[:, 0:6144].rearrange("p (j t) -> p j t", j=8)
    hb = r1[:, 0:6144].rearrange("p (t c) -> p t c", t=G)
    kra = r1[:, 0:512].bitcast(BF16)
    yc = r2.rearrange("p (j c) -> p j c", j=8)
    kT = r2[:, 0:3072].bitcast(BF16).rearrange("p (h t) -> p h t", h=8)
    rT = r2[:, 3072:6144].bitcast(BF16).rearrange("p (h t) -> p h t", h=8)
    sbg = r2[:, 0:6144].rearrange("p (j t) -> p j t", j=8)
    aT = [r2[:, 0:3072].bitcast(BF16).rearrange("p (i t) -> p i t", i=8),
          r2[:, 3072:6144].bitcast(BF16).rearrange("p (i t) -> p i t", i=8)]
    zT = r3.bitcast(BF16).rearrange("p (j t) -> p j t", j=8)
    hnT = zT
    ropeA = r3[:, 0:2048].rearrange("p (c t f) -> p c t f", c=2, t=NPT)
    xnT = r4.rearrange("p (k t) -> p k t", k=8)
    mixT = xnT
    xa = [r4[:, 0:1024].rearrange("p (k t) -> p k t", k=8),
          r4[:, 1024:2048].rearrange("p (k t) -> p k t", k=8)]
    tA, tB, tC, tD = scr[:, 0:512], scr[:, 512:1024], scr[:, 1024:1536], scr[:, 1536:2048]
    sqb = scr[:, 0:1024]
    onf = scr[:, 1024:2048]
    onb = scr[:, 2048:2560].bitcast(BF16)
    yo = scr[:, 0:1024]
    ost = scr[:, 1024:2048]

    for n, (reg, own) in {
        "U": ("R1", "U"), "qT": ("R1", "qk"), "kr": ("R1", "qk"), "ry": ("R1", "ry"), "h": ("R1", "h"),
        "kra": ("R1", "pa"),
        "yc": ("R2", "yc"), "kT": ("R2", "kTrT"), "rT": ("R2", "kTrT"), "sbg": ("R2", "sbg"), "aT": ("R2", "aT"),
        "z": ("R3", "z"), "hnT": ("R3", "hnT"), "ropeA": ("R3", "ropeA"),
        "xnT": ("R4", "xnT"), "mixT": ("R4", "mixT"), "xa": ("R4", "xa"),
        "tA": ("SCR", "rope"), "tB": ("SCR", "rope"), "tC": ("SCR", "rope"), "tD": ("SCR", "rope"),
        "sq": ("SCR", "ret"), "onf": ("SCR", "ret"), "onb": ("SCR", "ret"),
        "yo": ("SCR", "fin"), "ost": ("SCR", "fin"),
    }.items():
        S.set_alias(n, reg, own)

    gam = _gammas()
    G128 = [g ** 128 for g in gam]
    G64 = [g ** 64 for g in gam]

    bank_ctr = [0]
    pair_ctr = [0]

    tbank_ctr = [0]

    def next_bank():
        b = bank_ctr[0] % 6
        bank_ctr[0] += 1
        return b

    def next_tbank():
        b = 6 + tbank_ctr[0] % 2
        tbank_ctr[0] += 1
        return b

    def next_pair():
        p = pair_ctr[0] % 3
        pair_ctr[0] += 1
        return p

    def bank_ap(b):
        return ps[b // 2][:, (b % 2) * 512:(b % 2) * 512 + 512]

    def bank_bf(b):
        return ps[b // 2][:, (b % 2) * 512:(b % 2) * 512 + 512].bitcast(BF16)

    def pk(p):
        return [("ps", 2 * p), ("ps", 2 * p + 1)]

    def bk(b):
        return [("ps", b)]

    stc = [0]
    stmap = {}

    def stat(name, w=1):
        if name not in stmap:
            stmap[name] = (stc[0], w)
            stc[0] += w
            assert stc[0] <= 128
        o, w = stmap[name]
        return st[:, o:o + w]

    def tcol(off, w=8):
        return tabs[:, off:off + w]

    def bc(ap2, n):
        return ap2.unsqueeze(2).to_broadcast([128, ap2.shape[1], n])

    def v3(a, k=8):
        return a.rearrange("p (h f) -> p h f", h=k)

    def dma(eng, out, in_, reads, writes, key, final=False, nonc=False, record=True):
        def fn(h):
            if nonc:
                with nc.allow_non_contiguous_dma(reason="tiny strided state rows"):
                    return h.dma_start(out=out, in_=in_)
            return h.dma_start(out=out, in_=in_)
        return S.add(eng, fn, reads, writes, dma=key, final=final, record=record)

    def mm(out, lhsT, rhs, start, stop, reads, writes):
        S.add("pe", lambda h: h.matmul(out, lhsT, rhs, start=start, stop=stop), reads, writes)

    def tr(out, in_, idn, reads, writes):
        S.add("pe", lambda h: h.transpose(out, in_, idn), reads, writes)

    def act(out, in_, func, reads, writes, accum=None):
        def fn(h):
            if accum is not None:
                return h.activation(out=out, in_=in_, func=func, accum_out=accum)
            return h.activation(out=out, in_=in_, func=func)
        S.add("act", fn, reads, writes)

    def tt(eng, out, a, b, op, reads, writes):
        S.add(eng, lambda h: h.tensor_tensor(out=out, in0=a, in1=b, op=op), reads, writes)

    def ts(eng, out, a, s1, s2, op0, op1, reads, writes):
        if op1 is None:
            S.add(eng, lambda h: h.tensor_scalar(out=out, in0=a, scalar1=s1, scalar2=None, op0=op0), reads, writes)
        else:
            S.add(eng, lambda h: h.tensor_scalar(out=out, in0=a, scalar1=s1, scalar2=s2, op0=op0, op1=op1),
                  reads, writes)

    def stt(out, a, s, b, op0, op1, reads, writes):
        S.add("dve", lambda h: h.scalar_tensor_tensor(out=out, in0=a, scalar=s, in1=b, op0=op0, op1=op1),
              reads, writes)

    def cp(eng, out, in_, reads, writes):
        if eng == "act":
            S.add("act", lambda h: h.copy(out=out, in_=in_), reads, writes)
        else:
            S.add(eng, lambda h: h.tensor_copy(out=out, in_=in_), reads, writes)

    def recip(out, in_, reads, writes):
        S.add("dve", lambda h: h.reciprocal(out=out, in_=in_), reads, writes)

    def red(out, in_, reads, writes):
        S.add("dve", lambda h: h.tensor_reduce(out=out, in_=in_, axis=AX.X, op=ALU.add), reads, writes)

    dbg_out = {}

    def dump(name, ap, keys, shape, dt):
        if name not in dbg:
            return
        d = dout("dbg_" + name, shape, dt)
        dbg_out[name] = d
        dma("sp", d, ap, keys, [], "dbg_" + name, final=True)

    def ckpt(name):
        if stop == name:
            raise _Stop()

    dma("sp", tabs[:], tabs_d[:, :], [], [("tabs",)], "c0")
    dma("sp", identf[:], id_d[:, :], [], [("identf",)], "c1")
    dma("sp", maskb[:], mask_d[:, :, :, :], [], [("maskb",)], "c2")
    dma("sp", nfb[:], nf_d[:, :], [], [("nfb",)], "c3")
    cp("dve", ident[:], identf[:], [("identf",)], [("ident",)])
    S.add("pool", lambda h: h.memset(qTm[0][:], 0.0), [], [("qTm", 0)])
    S.add("pool", lambda h: h.memset(qTm[1][:], 0.0), [], [("qTm", 1)])
    S.add("pool", lambda h: h.memset(Sst[:], 0.0), [], [("S",)])

    def wsrc(name):
        if name in W_IN_ORDER:
            b = W_IN_ORDER.index(name)
            return win_d.rearrange("(kc p) n -> p kc n", p=128)[:, :, b * 1024:(b + 1) * 1024]
        if name == "ro":
            return wro_d.rearrange("(kc p) n -> p kc n", p=128)
        if name == "co":
            return wco_d.rearrange("(kc p) n -> p kc n", p=128)
        if name == "o":
            return wo_d.rearrange("(kc p) n -> p kc n", p=128)
        if name.startswith("f1"):
            j = int(name[2:])
            return wf1_d.rearrange("(kc p) n -> p kc n", p=128)[:, :, j * 1024:(j + 1) * 1024]
        j = int(name[2:])
        return wf2_d[j * 1024:(j + 1) * 1024, :].rearrange("(i p) n -> p i n", p=128)

    wseq = ["k", "v"] + STAGES * NG
    wload_ctr = [0]
    wuse_ctr = [0]

    def issue_wload():
        i = wload_ctr[0]
        if i >= len(wseq):
            return
        wload_ctr[0] += 1
        slot = i % 3
        src = wsrc(wseq[i])
        dma("pool", ring[slot][:, 0:4, :], src[:, 0:4, :], [], [("w", slot, 0), ("w", slot, 1)], "w%d" % slot,
            record=False)
        dma("pool", ring[slot][:, 4:8, :], src[:, 4:8, :], [], [("w", slot, 0), ("w", slot, 1)], "w%d" % slot)

    g_cur = [0]

    def stage_begin(name):
        ckpt("before_%s%d" % (name, g_cur[0]))
        i = wuse_ctr[0]
        assert wseq[i] == name, (wseq[i], name)
        wuse_ctr[0] += 1
        while wload_ctr[0] <= i + 2 and wload_ctr[0] < len(wseq):
            issue_wload()
        return i % 3

    def wk(slot):
        return [("w", slot, 0), ("w", slot, 1)]

    xs_ctr = [0]

    def load_x(src_rows, npart):
        s = xs_ctr[0] % 2
        xs_ctr[0] += 1
        dma("sp", xs[s][0:npart, :], src_rows, [], [("xs", s)], "xs%d" % s)
        return s

    def rms_stats(src_ap, src_keys, npart, tag, junk, junk_keys):
        ss, ms, sr, rs = stat("ss" + tag), stat("ms" + tag), stat("sr" + tag), stat("rs" + tag)
        act(junk, src_ap, AF.Square, src_keys, junk_keys + [("st", "ss" + tag)], accum=ss[0:npart, :])
        ts("dve", ms[0:npart, :], ss[0:npart, :], 1.0 / D, EPS, ALU.mult, ALU.add,
           [("st", "ss" + tag)], [("st", "ms" + tag)])
        act(sr[0:npart, :], ms[0:npart, :], AF.Sqrt, [("st", "ms" + tag)], [("st", "sr" + tag)])
        recip(rs[0:npart, :], sr[0:npart, :], [("st", "sr" + tag)], [("st", "rs" + tag)])
        return rs, ("st", "rs" + tag)

    def norm_scale(src_ap, src_keys, npart, rs, rs_key):
        ts("dve", xnb[0:npart, :], src_ap, rs[0:npart, 0:1], None, ALU.mult, None,
           src_keys + [rs_key], [("xnb",)])

    def norm_transpose(npart, gain_off, dst3, dst_keys, ncols=128):
        b = next_tbank()
        pb = bank_bf(b)
        for kc in range(8):
            tr(pb[:, kc * 128:kc * 128 + npart], xnb[0:npart, kc * 128:(kc + 1) * 128], ident[0:npart, 0:npart],
               [("xnb",), ("ident",)], bk(b))
        src3 = pb.rearrange("p (k t) -> p k t", k=8)[:, :, 0:ncols]
        tt("dve", dst3, src3, bc(tcol(gain_off), ncols), ALU.mult, bk(b) + [("tabs",)], dst_keys)

    def xn_tile(src_rows, npart, tag, dst3, dst_keys):
        s = load_x(src_rows, npart)
        rs, rk = rms_stats(xs[s][:], [("xs", s)], 128, tag, xnb[:], [("xnb",)])
        norm_scale(xs[s][:], [("xs", s)], 128, rs, rk)
        norm_transpose(128, T_N1, dst3, dst_keys, ncols=npart)

    def proj_tm(srcT3, cols, src_keys, slot, p):
        for hf in range(2):
            for kc in range(8):
                mm(ps[p][:, hf * 512:(hf + 1) * 512], srcT3[:, kc, cols], ring[slot][:, kc, hf * 512:(hf + 1) * 512],
                   kc == 0, kc == 7, src_keys + wk(slot), [("ps", 2 * p + hf)])

    def rope_tm(p, cos2, sin2, tab_keys, dst2, dst_keys):
        pv = ps[p][:].rearrange("p (h two f) -> p h two f", h=8, two=2)
        x1, x2 = pv[:, :, 0, :], pv[:, :, 1, :]
        cos = cos2.unsqueeze(1).to_broadcast([128, 8, 64])
        sin = sin2.unsqueeze(1).to_broadcast([128, 8, 64])
        d4 = dst2.rearrange("p (h two f) -> p h two f", h=8, two=2)
        tt("dve", v3(tA), x1, cos, ALU.mult, pk(p) + tab_keys, [("tA",)])
        tt("dve", v3(tB), x2, sin, ALU.mult, pk(p) + tab_keys, [("tB",)])
        tt("pool", d4[:, :, 0, :], v3(tA), v3(tB), ALU.subtract, [("tA",), ("tB",)], dst_keys)
        tt("dve", v3(tC), x2, cos, ALU.mult, pk(p) + tab_keys, [("tC",)])
        tt("dve", v3(tD), x1, sin, ALU.mult, pk(p) + tab_keys, [("tD",)])
        tt("pool", d4[:, :, 1, :], v3(tC), v3(tD), ALU.add, [("tC",), ("tD",)], dst_keys)

    def state_update(kr2, kr_keys, vd3, vd_keys, S3, S_keys, gl):
        p = next_pair()
        for h in range(H):
            mm(ps[p][:, h * 128:(h + 1) * 128], kr2[:, h * 128:(h + 1) * 128], vd3[:, h, :], True, True,
               kr_keys + vd_keys, [("ps", 2 * p + h // 4)])
        pv = v3(ps[p][:])
        for h in range(H):
            stt(S3[:, h, :], S3[:, h, :], float(gl[h]), pv[:, h, :], ALU.mult, ALU.add,
                [("ps", 2 * p + h // 4)] + S_keys, S_keys)

    def body():
        issue_wload()
        issue_wload()
        issue_wload()
        sk, sv = 0, 1
        wuse_ctr[0] = 2
        for t in range(NPREV):
            if t % NPT == 0:
                dma("sp", ropeA, ropeP_d[:, :, t:t + NPT, :], [], [("ropeA",)], "c4")
            xat = xa[t % 2]
            tq, tr_ = divmod(t, NPREV // NXP)
            xn_tile(xprev_d[tq][tr_ * 128:(tr_ + 1) * 128, :], 128, "a", xat, [("xa", t % 2)])
            p_k = next_pair()
            proj_tm(xat, slice(0, 128), [("xa", t % 2)], sk, p_k)
            rope_tm(p_k, ropeA[:, 0, t % NPT, :], ropeA[:, 1, t % NPT, :], [("ropeA",)], kra, [("kra",)])
            p_v = next_pair()
            proj_tm(xat, slice(0, 128), [("xa", t % 2)], sv, p_v)
            tt("dve", vdb[0][:], v3(ps[p_v][:]), bc(tcol(T_VP), 128), ALU.mult, pk(p_v) + [("tabs",)], [("vd", 0)])
            state_update(kra, [("kra",)], vdb[0], [("vd", 0)], Sst, [("S",)], G128)
        cp("act", Sbf[:], Sst[:], [("S",)], [("Sbf",)])
        dump("Sin", Sst[:].rearrange("p h e -> p (h e)"), [("S",)], [128, D], F32)
        ckpt("phaseA")
        issue_wload()
        issue_wload()

        def group_pieces(g, lo, n):
            if g < 2:
                return [(lo, n, "p", 2 + lo)]
            out = []
            a, b = lo, min(lo + n, 512)
            if a < b:
                out.append((a, b - a, "p", 2 + a))
            a, b = max(lo, 512), lo + n
            if a < b:
                out.append((a, b - a, "s", (a - 512) // 64))
            return out

        def uview(buf3, j, piece):
            lo, n, kind, c0 = piece
            if kind == "p":
                return buf3[:, j, c0:c0 + n]
            ns = n // 64
            return buf3[:, j, 514 + 66 * c0:514 + 66 * (c0 + ns)].rearrange("p (s c) -> p s c", c=66)[:, :, 2:66]

        def pview(ap2, piece, base):
            lo, n, kind, c0 = piece
            a = ap2[:, lo - base:lo - base + n]
            if kind == "p":
                return a
            return a.rearrange("p (s c) -> p s c", c=64)

        def stage_fm(slot, srcT3, src_keys, epi):
            for blk in range(2):
                lo = blk * 384
                for j in range(8):
                    b = next_bank()
                    for kc in range(8):
                        mm(bank_ap(b)[:, 0:384], ring[slot][:, kc, j * 128:(j + 1) * 128], srcT3[:, kc, lo:lo + 384],
                           kc == 0, kc == 7, src_keys(blk) + wk(slot), bk(b))
                    epi(j, blk, lo, b)

        def tkeys(name):
            return lambda blk: [(name, 3 * blk), (name, 3 * blk + 1), (name, 3 * blk + 2)]

        def bkeys(name):
            return lambda blk: [(name, blk)]

        tmp_ctr = [0]

        def next_tmp():
            k = tmp_ctr[0] % 2
            tmp_ctr[0] += 1
            return k

        for g in range(NG):
            tiles = list(range(g * G, (g + 1) * G))
            g_cur[0] = g
            mixed = (g == 2)
            np_tok = 512 if mixed else 768

            dma("sp", ropeb[:], rope_d[:, :, g * G:(g + 1) * G, :], [], [("rope",)], "c5")

            ckpt("g%d_rope" % g)
            if g == 0:
                xn_tile(xh_d[:, :], 2, "h", xnTh[:], [("xnTh",)])
            ckpt("g%d_halo" % g)
            for lt, t in enumerate(tiles):
                xn_tile(x_d[t * 128:(t + 1) * 128, :], 128, "x", xnT[:, :, lt * 128:(lt + 1) * 128], [("xnT", lt)])
                ckpt("g%d_x%d" % (g, lt))
            if g == 0:
                dump("xnT0", xnT.rearrange("p k t -> p (k t)"), [("xnT", i) for i in range(G)], [128, 6144], BF16)

            xk = tkeys("xnT")

            slot = stage_begin("cx")
            if g > 0:
                cp("act", U[:, :, 0:2], uh[:], [("uh",)], [("U", "h")])
            if mixed:
                for s_ in range(4):
                    c0 = 514 + 66 * s_
                    for r_ in range(2):
                        dma("sp", U[:, :, c0 + r_], sconv_d[s_, r_].rearrange("(j p) -> p j", p=128),
                            [], [("U", "hs%d" % s_)], "c6", nonc=True)

            def epi_cx(j, blk, lo, b):
                for pc in group_pieces(g, lo, 384):
                    cp("act", uview(U, j, pc), pview(bank_ap(b), pc, lo), bk(b), [("U", blk)])
            stage_fm(slot, xnT, xk, epi_cx)
            if g == 0:
                for j in range(8):
                    b = next_bank()
                    for kc in range(8):
                        mm(bank_ap(b)[:, 0:2], ring[slot][:, kc, j * 128:(j + 1) * 128], xnTh[:, kc, :],
                           kc == 0, kc == 7, [("xnTh",)] + wk(slot), bk(b))
                    cp("act", hx[:, j, :], bank_ap(b)[:, 0:2], bk(b), [("hx",)])

            slot = stage_begin("cc")

            def epi_cc(j, blk, lo, b):
                for pc in group_pieces(g, lo, 384):
                    tt("dve", uview(U, j, pc), pview(bank_ap(b), pc, lo), uview(U, j, pc), ALU.mult,
                       bk(b) + [("U", blk)], [("U", blk)])
            stage_fm(slot, xnT, xk, epi_cc)
            if g == 0:
                for j in range(8):
                    b = next_bank()
                    for kc in range(8):
                        mm(bank_ap(b)[:, 0:2], ring[slot][:, kc, j * 128:(j + 1) * 128], xnTh[:, kc, :],
                           kc == 0, kc == 7, [("xnTh",)] + wk(slot), bk(b))
                    tt("dve", U[:, j, 0:2], bank_ap(b)[:, 0:2], hx[:, j, :], ALU.mult, bk(b) + [("hx",)], [("U", "h")])
            ukeys = [("U", 0), ("U", 1), ("U", "h")] + ([("U", "hs%d" % s_) for s_ in range(4)] if mixed else [])

            def sview(buf3, j, shift):
                return buf3[:, j, 514:778].rearrange("p (s c) -> p s c", c=66)[:, :, 2 - shift:66 - shift]

            for j in range(8):
                cw = [tabs[:, T_CW + tap * 8 + j:T_CW + tap * 8 + j + 1] for tap in range(3)]
                views = [(lambda sh, j=j: U[:, j, 2 - sh:2 - sh + np_tok], yc[:, j, 2:2 + np_tok])]
                if mixed:
                    views.append((lambda sh, j=j: sview(U, j, sh), sview(yc, j, 0)))
                for uv, yv in views:
                    ts("dve", yv, uv(2), cw[0], None, ALU.mult, None, ukeys + [("tabs",)], [("yc",)])
                    stt(yv, uv(1), cw[1], yv, ALU.mult, ALU.add, ukeys + [("tabs",), ("yc",)], [("yc",)])
                    stt(yv, uv(0), cw[2], yv, ALU.mult, ALU.add, ukeys + [("tabs",), ("yc",)], [("yc",)])
            cp("act", uh[:], U[:, :, np_tok:np_tok + 2], ukeys, [("uh",)])
            if mixed:
                for r_ in range(2):
                    dma("sp", ncp_d[r_].rearrange("(j p) -> p j", p=128), U[:, :, 512 + r_], ukeys, [], "o_ncp",
                        final=True, nonc=True)
                for s_ in range(4):
                    c0 = 514 + 66 * s_ + 64
                    for r_ in range(2):
                        dma("sp", ncs_d[s_, r_].rearrange("(j p) -> p j", p=128), U[:, :, c0 + r_], ukeys, [],
                            "o_ncs", final=True, nonc=True)

            slot = stage_begin("cb")

            def epi_cb(j, blk, lo, b):
                for pc in group_pieces(g, lo, 384):
                    zv = zT[:, j, pc[0]:pc[0] + pc[1]]
                    if pc[2] == "s":
                        zv = zv.rearrange("p (s c) -> p s c", c=64)
                    tt("dve", zv, pview(bank_ap(b), pc, lo), uview(yc, j, pc), ALU.mult,
                       bk(b) + [("yc",)], [("z", blk)])
            stage_fm(slot, xnT, xk, epi_cb)
            if g == 0:
                dump("z0", zT.rearrange("p j t -> p (j t)"), [("z", 0), ("z", 1)], [128, 6144], BF16)

            def qk_stage(name, ci, si, dstT, dst_name, roped_of, roped_keys_of):
                slot_ = stage_begin(name)
                pending = None

                def post(lt):
                    b = next_tbank()
                    pb = bank_bf(b)
                    src = roped_of(lt)
                    for h in range(H):
                        tr(pb[:, h * 128:(h + 1) * 128], src[:, h * 128:(h + 1) * 128], ident[:],
                           roped_keys_of(lt) + [("ident",)], bk(b))
                    cp("act", dstT[:, :, lt * 128:(lt + 1) * 128], v3(pb), bk(b), [(dst_name, lt)])

                for lt in range(G):
                    p = next_pair()
                    proj_tm(xnT, slice(lt * 128, (lt + 1) * 128), [("xnT", lt)], slot_, p)
                    if pending is not None:
                        post(pending)
                    rope_tm(p, ropeb[:, ci, lt, :], ropeb[:, si, lt, :], [("rope",)], roped_of(lt), roped_keys_of(lt))
                    pending = lt
                post(pending)

            qk_stage("q", 0, 1, qT, "qT", lambda lt: xnb[:], lambda lt: [("xnb",)])
            qk_stage("k", 2, 3, kT, "kT", lambda lt: kr[:, lt, :], lambda lt: [("kr", lt)])
            if g == 0:
                dump("qT0", qT.rearrange("p h t -> p (h t)"), [("qT", i) for i in range(G)], [128, 6144], BF16)
                dump("kT0", kT.rearrange("p h t -> p (h t)"), [("kT", i) for i in range(G)], [128, 6144], BF16)

            slot = stage_begin("v")
            pending = None

            def ret_post(args):
                lt_, = args
                b = next_tbank()
                pb = bank_bf(b)
                for h in range(H):
                    tr(pb[:, h * 128:(h + 1) * 128], onb[:, h * 128:(h + 1) * 128], ident[:], [("onb",), ("ident",)], bk(b))
                tt("dve", rT[:, :, lt_ * 128:(lt_ + 1) * 128], v3(pb), bc(tcol(T_GNW), 128), ALU.mult,
                   bk(b) + [("tabs",)], [("rT", lt_)])

            for lt, t in enumerate(tiles):
                sample = t >= NPT
                cols = slice(lt * 128, (lt + 1) * 128)
                pA = next_pair()
                proj_tm(xnT, cols, [("xnT", lt)], slot, pA)
                vv = v3(ps[pA][:])
                if not sample:
                    tt("dve", vdb[0][:], vv, bc(tcol(T_VP), 128), ALU.mult, pk(pA) + [("tabs",)], [("vd", 0)])
                else:
                    for i, off in enumerate((T_VS, T_VA, T_VB)):
                        tt("dve", vdb[i][:], vv, bc(tcol(off), 128), ALU.mult, pk(pA) + [("tabs",)], [("vd", i)])
                    for hf in range(2):
                        sidx = 2 * (t - NPT) + hf
                        dma("sp", sst[hf][:], sret_d[sidx].rearrange("h d e -> d h e"), [], [("sst", hf)], "sst%d" % hf)
                        cp("act", sstb[hf][:], sst[hf][:], [("sst", hf)], [("sstb", hf)])
                        cp("pool", qTm[hf][:, :, hf * 64:(hf + 1) * 64],
                           qT[:, :, lt * 128 + hf * 64:lt * 128 + (hf + 1) * 64], [("qT", lt)], [("qTm", hf)])
                if g == 0 and lt == 0:
                    ckpt("v0_a")
                pB = next_pair()
                for h in range(H):
                    mm(ps[pB][:, h * 128:(h + 1) * 128], kT[:, h, cols], qT[:, h, cols], True, True,
                       [("kT", lt), ("qT", lt)], [("ps", 2 * pB + h // 4)])
                tt("dve", Pm[:], v3(ps[pB][:]), maskb[:, 1 if sample else 0, :, :], ALU.mult,
                   pk(pB) + [("maskb",)], [("Pm",)])
                if g == 0 and lt == 0:
                    ckpt("v0_b")
                pC = next_pair()
                for h in range(H):
                    o_ap = ps[pC][:, h * 128:(h + 1) * 128]
                    wkeys = [("ps", 2 * pC + h // 4)]
                    mm(o_ap, Pm[:, h, :], vdb[0][:, h, :], True, False, [("Pm",), ("vd", 0)], wkeys)
                    if not sample:
                        mm(o_ap, qT[:, h, cols], Sbf[:, h, :], False, True, [("qT", lt), ("Sbf",)], wkeys)
                    else:
                        mm(o_ap, qTm[0][:, h, :], sstb[0][:, h, :], False, False, [("qTm", 0), ("sstb", 0)], wkeys)
                        mm(o_ap, qTm[1][:, h, :], sstb[1][:, h, :], False, True, [("qTm", 1), ("sstb", 1)], wkeys)
                if g == 0 and lt == 0:
                    ckpt("v0_c")
                if not sample:
                    state_update(kr[:, lt, :], [("kr", lt)], vdb[0], [("vd", 0)], Sst, [("S",)], G128)
                    cp("act", Sbf[:], Sst[:], [("S",)], [("Sbf",)])
                    if t == NPT - 1:
                        dma("sp", nrp_d.rearrange("h d e -> d h e"), Sst[:], [("S",)], [], "o_nrp", final=True)
                else:
                    for hf in range(2):
                        sidx = 2 * (t - NPT) + hf
                        state_update(kr[:, lt, :], [("kr", lt)], vdb[1 + hf], [("vd", 1 + hf)], sst[hf],
                                     [("sst", hf)], G64)
                        dma("sp", nrs_d[sidx].rearrange("h d e -> d h e"), sst[hf][:], [("sst", hf)], [],
                            "o_nrs%d" % hf, final=True)
                if g == 0 and lt == 0:
                    ckpt("v0_d")
                if g == 0 and lt == 1:
                    ckpt("v1_post")
                oc = v3(ps[pC][:])
                act(sqb[:, 0:512], ps[pC][:, 0:512], AF.Square, [("ps", 2 * pC)], [("sq",)])
                act(sqb[:, 512:1024], ps[pC][:, 512:1024], AF.Square, [("ps", 2 * pC + 1)], [("sq",)])
                s1, s2, mean, m2, var, sd, rstd, nb = (stat(n_, 8) for n_ in
                                                       ("g_s1", "g_s2", "g_mean", "g_m2", "g_var", "g_sd", "g_rstd", "g_nb"))
                red(s1, oc, pk(pC), [("st", "g_s1")])
                red(s2, v3(sqb), [("sq",)], [("st", "g_s2")])
                ts("dve", mean, s1, 1.0 / 128, None, ALU.mult, None, [("st", "g_s1")], [("st", "g_mean")])
                tt("dve", m2, mean, mean, ALU.mult, [("st", "g_mean")], [("st", "g_m2")])
                stt(var, s2, 1.0 / 128, m2, ALU.mult, ALU.subtract, [("st", "g_s2"), ("st", "g_m2")], [("st", "g_var")])
                tt("dve", var, var, tcol(T_ES if sample else T_EP), ALU.add, [("st", "g_var"), ("tabs",)],
                   [("st", "g_var")])
                act(sd, var, AF.Sqrt, [("st", "g_var")], [("st", "g_sd")])
                recip(rstd, sd, [("st", "g_sd")], [("st", "g_rstd")])
                stt(nb, mean, -1.0, rstd, ALU.mult, ALU.mult, [("st", "g_mean"), ("st", "g_rstd")], [("st", "g_nb")])
                tt("dve", v3(onf), oc, bc(rstd, 128), ALU.mult, pk(pC) + [("st", "g_rstd")], [("onf",)])
                tt("dve", v3(onb), v3(onf), bc(nb, 128), ALU.add, [("onf",), ("st", "g_nb")], [("onb",)])
                if g == 0 and lt == 0:
                    ckpt("v0_e")
                if g == 0 and lt in (1, 2):
                    dump("onb%d" % lt, onb, [("onb",)], [128, 1024], BF16)
                    dump("Pm%d" % lt, Pm[:].rearrange("p h l -> p (h l)"), [("Pm",)], [128, 1024], BF16)
                    dump("vd%d" % lt, vdb[0][:].rearrange("p h l -> p (h l)"), [("vd", 0)], [128, 1024], BF16)
                    dump("sq%d" % lt, sqb, [("sq",)], [128, 1024], F32)
                ret_post((lt,))
                ckpt("vt%d_%d" % (g, lt))
            if g == 0:
                dump("rT0", rT.rearrange("p h t -> p (h t)"), [("rT", i) for i in range(G)], [128, 6144], BF16)

            slot = stage_begin("g")

            def epi_g(j, blk, lo, b):
                k = next_tmp()
                act(tmp[k][:], bank_ap(b)[:, 0:384], AF.Silu, bk(b), [("tmp", k)])
                tt("dve", rT[:, j, lo:lo + 384], tmp[k][:], rT[:, j, lo:lo + 384], ALU.mult,
                   [("tmp", k)] + tkeys("rT")(blk), tkeys("rT")(blk))
            stage_fm(slot, xnT, xk, epi_g)

            slot = stage_begin("ro")

            def epi_ro(j, blk, lo, b):
                cp("act", ry[:, j, lo:lo + 384], bank_ap(b)[:, 0:384], bk(b), [("ry", blk)])
            stage_fm(slot, rT, tkeys("rT"), epi_ro)

            slot = stage_begin("ga")

            def epi_ga(j, blk, lo, b):
                k = next_tmp()
                act(tmp[k][:], bank_ap(b)[:, 0:384], AF.Sigmoid, bk(b), [("tmp", k)])
                tt("dve", ry[:, j, lo:lo + 384], tmp[k][:], ry[:, j, lo:lo + 384], ALU.mult,
                   [("tmp", k), ("ry", blk)], [("ry", blk)])
            stage_fm(slot, xnT, xk, epi_ga)

            slot = stage_begin("gb")

            def epi_gb(j, blk, lo, b):
                act(sbg[:, j, lo:lo + 384], bank_ap(b)[:, 0:384], AF.Sigmoid, bk(b), [("sbg", blk)])
            stage_fm(slot, xnT, xk, epi_gb)

            slot = stage_begin("co")

            def epi_co(j, blk, lo, b):
                k = next_tmp()
                tt("dve", tmp[k][:], bank_ap(b)[:, 0:384], sbg[:, j, lo:lo + 384], ALU.mult,
                   bk(b) + [("sbg", blk)], [("tmp", k)])
                tt("pool", mixT[:, j, lo:lo + 384], tmp[k][:], ry[:, j, lo:lo + 384], ALU.add,
                   [("tmp", k), ("ry", blk)], [("mixT", blk)])
            stage_fm(slot, zT, bkeys("z"), epi_co)
            if g == 0:
                dump("mix0", mixT.rearrange("p j t -> p (j t)"), [("mixT", 0), ("mixT", 1)], [128, 6144], BF16)

            slot = stage_begin("o")
            pending = None

            def o_post(lt_):
                norm_transpose(128, T_N2, hnT[:, :, lt_ * 128:(lt_ + 1) * 128], [("hnT", lt_)])

            for lt, t in enumerate(tiles):
                p = next_pair()
                proj_tm(mixT, slice(lt * 128, (lt + 1) * 128), [("mixT", lt // 3)], slot, p)
                if pending is not None:
                    o_post(pending)
                s = load_x(x_d[t * 128:(t + 1) * 128, :], 128)
                tt("dve", hb[:, lt, :], ps[p][:], xs[s][:], ALU.add, pk(p) + [("xs", s)], [("h", lt)])
                rs, rk = rms_stats(hb[:, lt, :], [("h", lt)], 128, "o", xnb[:], [("xnb",)])
                norm_scale(hb[:, lt, :], [("h", lt)], 128, rs, rk)
                pending = lt
            o_post(pending)
            if g == 0:
                dump("h0", hb.rearrange("p t c -> p (t c)"), [("h", i) for i in range(G)], [128, 6144], F32)
                dump("hnT0", hnT.rearrange("p j t -> p (j t)"), [("hnT", i) for i in range(G)], [128, 6144], BF16)

            for jf in range(4):
                slot = stage_begin("f1%d" % jf)
                aTj = aT[jf % 2]

                def epi_f1(i, blk, lo, b, aTj=aTj, jf=jf):
                    k = next_tmp()
                    act(tmp[k][:], bank_ap(b)[:, 0:384], AF.Relu, bk(b), [("tmp", k)])
                    tt("pool", aTj[:, i, lo:lo + 384], tmp[k][:], tmp[k][:], ALU.mult, [("tmp", k)],
                       [("aT", jf % 2, blk)])
                stage_fm(slot, hnT, tkeys("hnT"), epi_f1)

                slot = stage_begin("f2%d" % jf)
                for lt, t in enumerate(tiles):
                    p = next_pair()
                    proj_tm(aTj, slice(lt * 128, (lt + 1) * 128), [("aT", jf % 2, lt // 3)], slot, p)
                    if jf < 3:
                        tt("dve", hb[:, lt, :], ps[p][:], hb[:, lt, :], ALU.add, pk(p) + [("h", lt)], [("h", lt)])
                    else:
                        tt("dve", yo, ps[p][:], hb[:, lt, :], ALU.add, pk(p) + [("h", lt)], [("yo",)])
                        rs, rk = rms_stats(yo, [("yo",)], 128, "f", ost, [("ost",)])
                        stt(ost, yo, rs[:, 0:1], nfb[:], ALU.mult, ALU.mult, [("yo",), rk, ("nfb",)], [("ost",)])
                        dma("sp", y_d[t * 128:(t + 1) * 128, :], ost, [("ost",)], [], "yout", final=True)

    try:
        body()
    except _Stop:
        pass

    S.emit(nc, es)
    es.close()
    build.last_sched = S
    return nc, dbg_out


def _const_tables(core):
    gam = np.array(_gammas(), dtype=np.float64)
    lg = np.log(gam)
    m = np.arange(128, dtype=np.float64)
    tabs = np.zeros((128, NTAB), dtype=np.float64)
    tabs[:, T_VP:T_VP + 8] = np.exp(lg[None, :] * (127.0 - m[:, None]))
    vs = np.exp(lg[None, :] * (63.0 - (m[:, None] % 64)))
    tabs[:, T_VS:T_VS + 8] = vs
    tabs[:, T_VA:T_VA + 8] = vs * (m[:, None] < 64)
    tabs[:, T_VB:T_VB + 8] = vs * (m[:, None] >= 64)
    tabs[:, T_EP:T_EP + 8] = EPS * np.exp(lg[None, :] * (-2.0 * (m[:, None] + 1.0)))
    tabs[:, T_ES:T_ES + 8] = EPS * np.exp(lg[None, :] * (-2.0 * ((m[:, None] % 64) + 1.0)))
    for c in range(NCORES):
        if c < core:
            tabs[:, T_COEF + c * 8:T_COEF + c * 8 + 8] = np.exp(lg * 2048.0 * (core - 1 - c))[None, :]
    mm_ = m[:, None]
    ll = m[None, :]
    masks = np.zeros((128, 2, H, 128), dtype=np.float64)
    same = (mm_ // 64) == (ll // 64)
    causal_x = (ll >= 64) & (mm_ < 64)
    for h in range(H):
        Mp = np.where(same, np.exp(lg[h] * np.abs(ll - mm_)), np.where(causal_x, np.exp(lg[h] * (ll - mm_)), 0.0))
        masks[:, 0, h, :] = Mp * np.exp(lg[h] * (-(127.0 - mm_) - (ll + 1.0)))
        i, j = ll % 64, mm_ % 64
        Ms = np.where(same, np.exp(lg[h] * np.abs(i - j)), 0.0)
        masks[:, 1, h, :] = Ms * np.exp(lg[h] * (-(63.0 - j) - (i + 1.0)))
    half = 64
    inv = np.exp(np.float32(-np.log(ROPE_BASE)) * np.arange(half, dtype=np.float32) / np.float32(half)).astype(np.float32)
    p = np.arange(128)
    pos = np.zeros((128, NT), dtype=np.float32)
    for t in range(NPT):
        pos[:, t] = core * 2048 + t * 128 + p
    for t in range(NPT, NT):
        pos[:, t] = PAST + (p % 64)
    ang = (pos[:, :, None] * inv[None, None, :]).astype(np.float32)
    cos, sin = np.cos(ang).astype(np.float32), np.sin(ang).astype(np.float32)
    ksc = np.float32(128.0 ** -0.5)
    rope = np.stack([cos, sin, cos * ksc, sin * ksc], axis=1).astype(np.float32)
    posp = (core * 2048 - 7 * 2048 + np.arange(7 * NPT)[None, :] * 128 + p[:, None]).astype(np.float32)
    posp = np.maximum(posp, np.float32(0.0))
    angp = (posp[:, :, None] * inv[None, None, :]).astype(np.float32)
    ropeP = np.stack([np.cos(angp).astype(np.float32) * ksc, np.sin(angp).astype(np.float32) * ksc],
                     axis=1).astype(np.float32)
    return tabs, masks.astype(np.float32), rope, ropeP


_PROG = {}


def _get_prog(dbg=(), stop=None):
    key = (tuple(sorted(dbg)), stop)
    if key not in _PROG:
        _PROG[key] = build(dbg, stop)
    return _PROG[key]


def _make_in_maps(x_prompt, x_sample, state_ret, state_conv, norm1, w_in, ret_gn_w, conv_w, w_ret_out, w_conv_out,
                  w_o, norm2, w_ff1, w_ff2, norm_f):
    f32 = np.float32
    xp = np.asarray(x_prompt, f32)[0]
    xsm = np.asarray(x_sample, f32)
    sret = np.asarray(state_ret, f32)[0]
    sconv = np.asarray(state_conv, f32)[0]
    w_in0 = np.ascontiguousarray(np.asarray(w_in, f32)[0])
    w_ro0 = np.ascontiguousarray(np.asarray(w_ret_out, f32)[0])
    w_co0 = np.ascontiguousarray(np.asarray(w_conv_out, f32)[0])
    w_o0 = np.ascontiguousarray(np.asarray(w_o, f32)[0])
    w_f10 = np.ascontiguousarray(np.asarray(w_ff1, f32)[0])
    w_f20 = np.ascontiguousarray(np.asarray(w_ff2, f32)[0])
    n1 = np.asarray(norm1, f32)[0]
    n2 = np.asarray(norm2, f32)[0]
    gnw = np.asarray(ret_gn_w, f32)[0]
    cw = np.asarray(conv_w, f32)[0]
    nf = np.ascontiguousarray(np.broadcast_to(np.asarray(norm_f, f32)[None, :], (128, D)))
    identf = np.eye(128, dtype=f32)

    in_maps = []
    for c in range(NCORES):
        tabs, masks, rope, ropeP = _const_tables(c)
        tabs = tabs.astype(f32)
        tabs[:, T_N1:T_N1 + 8] = n1.reshape(8, 128).T
        tabs[:, T_N2:T_N2 + 8] = n2.reshape(8, 128).T
        tabs[:, T_GNW:T_GNW + 8] = gnw.reshape(8, 128).T
        for tap in range(3):
            tabs[:, T_CW + tap * 8:T_CW + tap * 8 + 8] = cw[tap].reshape(8, 128).T
        xc = np.concatenate([xp[c * 2048:(c + 1) * 2048], xsm[4 * c:4 * c + 4].reshape(256, D)], axis=0)
        xh = xp[c * 2048 - 2:c * 2048] if c > 0 else np.zeros((2, D), f32)
        xprev = np.zeros((7 * 2048, D), f32)
        if c > 0:
            xprev[7 * 2048 - c * 2048:] = xp[:c * 2048]
        in_maps.append({
            "x": np.ascontiguousarray(xc), "xh": np.ascontiguousarray(xh),
            "sret": np.ascontiguousarray(sret[4 * c:4 * c + 4]), "sconv": np.ascontiguousarray(sconv[4 * c:4 * c + 4]),
            "w_in": w_in0, "w_ro": w_ro0, "w_co": w_co0, "w_o": w_o0, "w_ff1": w_f10, "w_ff2": w_f20,
            "tabs": np.ascontiguousarray(tabs), "rope": rope, "masks": masks, "nf": nf, "identf": identf,
            "xprev0": xprev[0:3584], "xprev1": xprev[3584:7168], "xprev2": xprev[7168:10752],
            "xprev3": xprev[10752:14336], "ropeP": np.ascontiguousarray(ropeP),
        })
    return in_maps


def kernel(x_prompt, x_sample, state_ret, state_conv, norm1, w_in, ret_gn_w, conv_w, w_ret_out, w_conv_out,
           w_o, norm2, w_ff1, w_ff2, norm_f, _dbg=(), _stop=None):
    f32 = np.float32
    in_maps = _make_in_maps(x_prompt, x_sample, state_ret, state_conv, norm1, w_in, ret_gn_w, conv_w, w_ret_out,
                            w_conv_out, w_o, norm2, w_ff1, w_ff2, norm_f)
    nc, dbg_out = _get_prog(_dbg, _stop)
    res = run_bass_kernel_spmd(nc, in_maps, core_ids=list(range(NCORES)))
    R = res.results
    y_prompt = np.concatenate([np.asarray(R[c]["y"], f32)[:2048] for c in range(NCORES)], axis=0)[None]
    y_sample = np.concatenate([np.asarray(R[c]["y"], f32)[2048:].reshape(4, DEC_S, D) for c in range(NCORES)], axis=0)
    new_ret_prompt = np.asarray(R[NCORES - 1]["nrp"], f32)[None, None]
    new_conv_prompt = np.asarray(R[NCORES - 1]["ncp"], f32)[None, None]
    new_ret_sample = np.concatenate([np.asarray(R[c]["nrs"], f32) for c in range(NCORES)], axis=0)[None]
    new_conv_sample = np.concatenate([np.asarray(R[c]["ncs"], f32) for c in range(NCORES)], axis=0)[None]
    out = (y_prompt, y_sample, new_ret_prompt, new_conv_prompt, new_ret_sample, new_conv_sample)
    if _dbg:
        return out, R
    return out
```
